# Optimizing a Trainium2 kernel written in Bass

```python
import jax, jax.numpy as jnp
from jax import lax
import numpy as np

D_MODEL = 2048
BATCH = 4
SEQ = 8192
DEPTH = 1
DEC_BATCH = 16
DEC_SEQ = 32
PAST_LEN = 2048

CHUNK = 64
EPS = 1e-6
D_FF = 5632
D_A = D_MODEL
A_HEADS = 8
A_HEAD_DIM = D_A // A_HEADS
MLP_CHUNK = 128
D_B = D_MODEL
SSM_HEADS = 32
SSM_HEAD_DIM = D_B // SSM_HEADS
SSM_GROUPS = 4
HEADS_PER_GROUP = SSM_HEADS // SSM_GROUPS
D_STATE = 128
CONV_W = 4
CONV_DIM = D_B + 2 * SSM_GROUPS * D_STATE
D_MIX = D_A + D_B
D_IN = 2 * D_A + D_B + CONV_DIM + SSM_HEADS

kernel_name = "hybrid_gmlp_ssd_macaron_stream_step"


def rms_norm(x, g):
    x32 = x.astype(jnp.float32)
    y = x32 * lax.rsqrt(jnp.mean(x32 * x32, axis=-1, keepdims=True) + EPS)
    return (y * g.astype(jnp.float32)).astype(x.dtype)


def swiglu(x, w_gate, w_up, w_down):
    return (jax.nn.silu(x @ w_gate) * (x @ w_up)) @ w_down


def spatial_gate(v_n, w_s, b_s):
    b, L, H, dh = v_n.shape
    lc = min(L, MLP_CHUNK)
    mask = jnp.tril(jnp.ones((lc, lc), dtype=bool))
    wm = jnp.where(mask[None], w_s[:, :lc, :lc], 0.0).astype(v_n.dtype)
    vc = v_n.reshape(b, L // lc, lc, H, dh)
    out = jnp.einsum('hts,bcshd->bcthd', wm, vc) + b_s[:, :lc].T.astype(v_n.dtype)[None, None, :, :, None]
    return out.reshape(b, L, H, dh)


def causal_conv(xbc, prev, w, bias):
    L = xbc.shape[1]
    xpad = jnp.concatenate([prev.astype(xbc.dtype), xbc], axis=1)
    y = bias.astype(xbc.dtype)
    for k in range(CONV_W):
        y = y + xpad[:, k:k + L] * w[k].astype(xbc.dtype)
    return jax.nn.silu(y), xpad[:, -(CONV_W - 1):]


def ssd_block(state, inputs):
    xdt, dA, Bm, Cm = inputs
    L = xdt.shape[1]
    acs = jnp.cumsum(dA, axis=1)
    seg = acs[:, :, None] - acs[:, None, :]
    mask = jnp.tril(jnp.ones((L, L), dtype=bool))[None, :, :, None, None]
    decay = jnp.exp(jnp.where(mask, seg, -jnp.inf))
    cb = jnp.einsum('blgn,bsgn->blsg', Cm, Bm)
    y_diag = jnp.einsum('blsg,blsgk,bsgkp->blgkp', cb, decay, xdt)
    y_off = jnp.einsum('blgn,bgkpn->blgkp', Cm, state) * jnp.exp(acs)[..., None]
    decay_end = jnp.exp(acs[:, -1:] - acs)
    new_state = (jnp.exp(acs[:, -1])[..., None, None] * state
                 + jnp.einsum('bsgn,bsgk,bsgkp->bgkpn', Bm, decay_end, xdt))
    return new_state, y_diag + y_off


def ssd(xdt, dA, Bm, Cm, state0):
    b, L = xdt.shape[:2]
    if L <= CHUNK:
        return ssd_block(state0, (xdt, dA, Bm, Cm))
    nc = L // CHUNK

    def to_chunks(t):
        return jnp.moveaxis(t.reshape(b, nc, CHUNK, *t.shape[2:]), 1, 0)

    state, ys = lax.scan(ssd_block, state0, (to_chunks(xdt), to_chunks(dA), to_chunks(Bm), to_chunks(Cm)))
    y = jnp.moveaxis(ys, 0, 1).reshape(b, L, *ys.shape[3:])
    return state, y


def layer(x, conv_prev, ssm_prev, p):
    b, L, _ = x.shape
    x = x + 0.5 * swiglu(rms_norm(x, p['ffn1_norm']), p['ffn1_w_gate'], p['ffn1_w_up'], p['ffn1_w_down'])
    h = rms_norm(x, p['mix_norm'])
    proj = h @ p['w_in']
    o = 0
    u = proj[..., o:o + D_A]; o += D_A
    v = proj[..., o:o + D_A]; o += D_A
    z = proj[..., o:o + D_B]; o += D_B
    xbc = proj[..., o:o + CONV_DIM]; o += CONV_DIM
    dt_raw = proj[..., o:o + SSM_HEADS]

    v_n = rms_norm(v.reshape(b, L, A_HEADS, A_HEAD_DIM), p['gmlp_v_norm'].reshape(A_HEADS, A_HEAD_DIM))
    a = u.reshape(b, L, A_HEADS, A_HEAD_DIM) * spatial_gate(v_n, p['gmlp_w_s'], p['gmlp_b_s'])
    a = rms_norm(a.reshape(b, L, D_A), p['gmlp_out_norm'])

    xbc, new_conv = causal_conv(xbc, conv_prev, p['conv_w'], p['conv_b'])
    xs = xbc[..., :D_B].reshape(b, L, SSM_GROUPS, HEADS_PER_GROUP, SSM_HEAD_DIM)
    Bm = xbc[..., D_B:D_B + SSM_GROUPS * D_STATE].reshape(b, L, SSM_GROUPS, D_STATE).astype(jnp.float32)
    Cm = xbc[..., D_B + SSM_GROUPS * D_STATE:].reshape(b, L, SSM_GROUPS, D_STATE).astype(jnp.float32)
    dt = jax.nn.softplus((dt_raw + p['dt_bias']).astype(jnp.float32)).reshape(b, L, SSM_GROUPS, HEADS_PER_GROUP)
    A = -jnp.exp(p['a_log'].astype(jnp.float32)).reshape(SSM_GROUPS, HEADS_PER_GROUP)
    xdt = xs.astype(jnp.float32) * dt[..., None]
    state0 = ssm_prev.astype(jnp.float32).reshape(b, SSM_GROUPS, HEADS_PER_GROUP, SSM_HEAD_DIM, D_STATE)
    new_ssm, y = ssd(xdt, dt * A, Bm, Cm, state0)
    y = y + p['d_skip'].astype(jnp.float32).reshape(SSM_GROUPS, HEADS_PER_GROUP)[..., None] * xs.astype(jnp.float32)
    y = y.reshape(b, L, D_B) * jax.nn.silu(z.astype(jnp.float32))
    y = rms_norm(y.reshape(b, L, SSM_GROUPS, D_B // SSM_GROUPS),
                 p['ssm_out_norm'].reshape(SSM_GROUPS, D_B // SSM_GROUPS)).reshape(b, L, D_B).astype(x.dtype)

    x = x + jnp.concatenate([a, y], axis=-1) @ p['w_out']
    x = x + 0.5 * swiglu(rms_norm(x, p['ffn2_norm']), p['ffn2_w_gate'], p['ffn2_w_up'], p['ffn2_w_down'])
    new_ssm = new_ssm.reshape(b, SSM_HEADS, SSM_HEAD_DIM, D_STATE)
    return x, v_n.reshape(b, L, D_A), new_conv, new_ssm


def setup_inputs(seed: int = 0) -> dict:
    key = jax.random.key(seed)
    ks = jax.random.split(key, 32)
    nrm = jax.random.normal

    def gain(k, shape):
        return 1.0 + 0.05 * nrm(k, shape, jnp.float32)

    dt0 = jnp.exp(jax.random.uniform(ks[14], (DEPTH, SSM_HEADS), jnp.float32,
                                     np.log(1e-3).astype(np.float32), np.log(1e-1).astype(np.float32)))
    dt_bias = dt0 + jnp.log(-jnp.expm1(-dt0))
    return {
        "x_prompt": nrm(ks[0], (BATCH, SEQ, D_MODEL), jnp.float32),
        "x_sample": nrm(ks[1], (DEC_BATCH, DEC_SEQ, D_MODEL), jnp.float32),
        "state_conv": nrm(ks[2], (DEPTH, DEC_BATCH, CONV_W - 1, CONV_DIM), jnp.float32),
        "state_ssm": 0.1 * nrm(ks[3], (DEPTH, DEC_BATCH, SSM_HEADS, SSM_HEAD_DIM, D_STATE), jnp.float32),
        "ffn1_norm": gain(ks[4], (DEPTH, D_MODEL)),
        "ffn1_w_gate": nrm(ks[5], (DEPTH, D_MODEL, D_FF), jnp.float32) * D_MODEL ** -0.5,
        "ffn1_w_up": nrm(ks[6], (DEPTH, D_MODEL, D_FF), jnp.float32) * D_MODEL ** -0.5,
        "ffn1_w_down": nrm(ks[7], (DEPTH, D_FF, D_MODEL), jnp.float32) * D_FF ** -0.5,
        "mix_norm": gain(ks[8], (DEPTH, D_MODEL)),
        "w_in": nrm(ks[9], (DEPTH, D_MODEL, D_IN), jnp.float32) * D_MODEL ** -0.5,
        "gmlp_v_norm": gain(ks[10], (DEPTH, D_A)),
        "gmlp_w_s": nrm(ks[11], (DEPTH, A_HEADS, MLP_CHUNK, MLP_CHUNK), jnp.float32) * MLP_CHUNK ** -0.5,
        "gmlp_b_s": 1.0 + 0.1 * nrm(ks[12], (DEPTH, A_HEADS, MLP_CHUNK), jnp.float32),
        "gmlp_out_norm": gain(ks[13], (DEPTH, D_A)),
        "conv_w": nrm(ks[15], (DEPTH, CONV_W, CONV_DIM), jnp.float32) * CONV_W ** -0.5,
        "conv_b": 0.02 * nrm(ks[16], (DEPTH, CONV_DIM), jnp.float32),
        "dt_bias": dt_bias,
        "a_log": jnp.log(jax.random.uniform(ks[17], (DEPTH, SSM_HEADS), jnp.float32, 1.0, 16.0)),
        "d_skip": gain(ks[18], (DEPTH, SSM_HEADS)),
        "ssm_out_norm": gain(ks[19], (DEPTH, D_B)),
        "w_out": nrm(ks[20], (DEPTH, D_MIX, D_MODEL), jnp.float32) * D_MIX ** -0.5,
        "ffn2_norm": gain(ks[21], (DEPTH, D_MODEL)),
        "ffn2_w_gate": nrm(ks[22], (DEPTH, D_MODEL, D_FF), jnp.float32) * D_MODEL ** -0.5,
        "ffn2_w_up": nrm(ks[23], (DEPTH, D_MODEL, D_FF), jnp.float32) * D_MODEL ** -0.5,
        "ffn2_w_down": nrm(ks[24], (DEPTH, D_FF, D_MODEL), jnp.float32) * D_FF ** -0.5,
        "final_norm": gain(ks[25], (D_MODEL,)),
    }


def reference(x_prompt, x_sample, state_conv, state_ssm, ffn1_norm, ffn1_w_gate, ffn1_w_up, ffn1_w_down,
              mix_norm, w_in, gmlp_v_norm, gmlp_w_s, gmlp_b_s, gmlp_out_norm, conv_w, conv_b, dt_bias,
              a_log, d_skip, ssm_out_norm, w_out, ffn2_norm, ffn2_w_gate, ffn2_w_up, ffn2_w_down, final_norm):
    bp = x_prompt.shape[0]
    xp, xs = x_prompt, x_sample
    p_conv, p_ssm, s_v, s_conv, s_ssm = [], [], [], [], []
    for l in range(DEPTH):
        p = {
            'ffn1_norm': ffn1_norm[l], 'ffn1_w_gate': ffn1_w_gate[l], 'ffn1_w_up': ffn1_w_up[l],
            'ffn1_w_down': ffn1_w_down[l], 'mix_norm': mix_norm[l], 'w_in': w_in[l],
            'gmlp_v_norm': gmlp_v_norm[l], 'gmlp_w_s': gmlp_w_s[l], 'gmlp_b_s': gmlp_b_s[l],
            'gmlp_out_norm': gmlp_out_norm[l], 'conv_w': conv_w[l], 'conv_b': conv_b[l],
            'dt_bias': dt_bias[l], 'a_log': a_log[l], 'd_skip': d_skip[l], 'ssm_out_norm': ssm_out_norm[l],
            'w_out': w_out[l], 'ffn2_norm': ffn2_norm[l], 'ffn2_w_gate': ffn2_w_gate[l],
            'ffn2_w_up': ffn2_w_up[l], 'ffn2_w_down': ffn2_w_down[l],
        }
        conv0 = jnp.zeros((bp, CONV_W - 1, CONV_DIM), xp.dtype)
        ssm0 = jnp.zeros((bp, SSM_HEADS, SSM_HEAD_DIM, D_STATE), jnp.float32)
        xp, _, pc, ps = layer(xp, conv0, ssm0, p)
        xs, sv, sc, ss = layer(xs, state_conv[l], state_ssm[l], p)
        p_conv.append(pc); p_ssm.append(ps); s_v.append(sv); s_conv.append(sc); s_ssm.append(ss)
    y_prompt = rms_norm(xp, final_norm)
    y_sample = rms_norm(xs, final_norm)
    return (y_prompt, y_sample, jnp.stack(p_conv), jnp.stack(p_ssm), jnp.stack(s_v), jnp.stack(s_conv), jnp.stack(s_ssm))
```

```python
import numpy as np
import concourse.bass as bass
import concourse.mybir as mybir
from concourse.bass_utils import run_bass_kernel_spmd

F32 = mybir.dt.float32
BF16 = mybir.dt.bfloat16
AF = mybir.ActivationFunctionType
ALU = mybir.AluOpType

D = 2048
KC = 16
FF = 5632
FC = 44
DIN = 9248
NH = 32
HD = 64
NG = 4
NCC = 24
EPS = 1e-6
TW = 256
SC = 512
NSLOT = 3

P_GCOL = 0
P_CONV = 80
P_BCOL = 200
P_DTB = 208
P_ALOG = 240
P_DSK = 272
P_ID = 304
P_TRIU = 432
P_ONES = 560
P_WST = 688
P_GFIN = 1712
P_GV = 3760
NPAR = 5808
G_FFN1, G_MIX, G_FFN2, G_GOUT, G_SSM = range(5)


class _Op:
    __slots__ = ("eng", "fn", "idx", "sig", "waits", "group", "gval", "sigcount")


class Prog:
    ENG = ("pe", "act", "dve", "pool", "sp")

    def __init__(self):
        self.streams = {e: [] for e in self.ENG}
        self.last_w = {}
        self.readers = {}
        self.seen = {e: {} for e in self.ENG}
        self.gcount = {}
        self.glast = {}

    def _add_waits(self, o, eng, deps):
        seen = self.seen[eng]
        for d in deps:
            if d.group is not None:
                key = ("g", d.group)
                val = d.gval
            else:
                if d.eng == "pe" and eng == "pe":
                    continue
                key = ("e", d.eng)
                val = d.idx
            if seen.get(key, -1) >= val:
                continue
            seen[key] = val
            o.waits.append(d)
            if d.group is None:
                d.sig = True

    def op(self, eng, fn, r=(), w=(), group=None, ndma=1, extra=()):
        o = _Op()
        o.eng = eng
        o.fn = fn
        o.sig = False
        o.waits = []
        o.group = group
        o.gval = 0
        st = self.streams[eng]
        o.idx = len(st)
        if group is not None:
            c = self.gcount.get(group, 0) + ndma
            self.gcount[group] = c
            o.gval = c * 16
            self.glast[group] = o
        deps = list(extra)
        for t in r:
            d = self.last_w.get(t)
            if d is not None:
                deps.append(d)
        for t in w:
            d = self.last_w.get(t)
            if d is not None:
                deps.append(d)
            rd = self.readers.get(t)
            if rd:
                deps.extend(rd.values())
        self._add_waits(o, eng, deps)
        for t in w:
            self.last_w[t] = o
            self.readers[t] = {}
        rk = ("g", group) if group is not None else ("e", eng)
        for t in r:
            if t in w:
                continue
            self.readers.setdefault(t, {})[rk] = o
        st.append(o)
        return o

    def barrier(self):
        lasts = [st[-1] for st in self.streams.values() if st]
        lasts = [o for o in lasts if o.group is None]
        glasts = list(self.glast.values())
        for e in self.ENG:
            self.op(e, lambda en: en.nop(), extra=[o for o in lasts if o.eng != e] + glasts)

    def finish(self):
        self.op("sp", lambda en: en.nop(), extra=list(self.glast.values()))

    def emit(self, nc):
        from contextlib import ExitStack
        for e, st in self.streams.items():
            c = 0
            for o in st:
                if o.sig:
                    c += 1
                o.sigcount = c
        with ExitStack() as es:
            esem = {e: es.enter_context(nc.semaphore("s_" + e)) for e in self.ENG}
            gsem = {g: es.enter_context(nc.semaphore("g_" + str(g))) for g in self.gcount}
            block = es.enter_context(nc.Block())

            def run(engname):
                def body(en):
                    for o in self.streams[engname]:
                        for d in o.waits:
                            if d.group is not None:
                                en.wait_ge(gsem[d.group], d.gval)
                            else:
                                en.wait_ge(esem[d.eng], d.sigcount)
                        ins = o.fn(en)
                        if o.group is not None:
                            for i in ins:
                                i.then_inc(gsem[o.group], 16)
                        elif o.sig:
                            ins.then_inc(esem[engname], 1)
                return body

            block.tensor(run("pe"))
            block.scalar(run("act"))
            block.vector(run("dve"))
            block.gpsimd(run("pool"))
            block.sync(run("sp"))


class Buf:
    def __init__(self, handle, ftot, base):
        self.h = handle
        self.F = ftot
        self.base = base

    def ap(self, off, dims, np_=128, p0=0):
        return bass.AP(self.h, p0 * self.F + self.base + off, [[self.F, np_]] + [[s, c] for s, c in dims])


class Mem:
    def __init__(self, nc, nbytes):
        self.hb = nc.alloc_sbuf_tensor("big", [128, nbytes // 2], BF16)
        self.hf = self.hb.bitcast(F32)
        self.nbytes = nbytes
        self.top = 0

    def alloc(self, nelem, dtype, at=None):
        sz = 4 if dtype == F32 else 2
        nb = (nelem * sz + 31) // 32 * 32
        if at is None:
            at = self.top
            self.top += nb
        assert at + nb <= self.nbytes, ("SBUF overflow", at, nb, self.nbytes)
        if dtype == F32:
            return Buf(self.hf, self.nbytes // 4, at // 4)
        return Buf(self.hb, self.nbytes // 2, at // 2)


DBG = []


def build(npt, npre=0, sample=True):
    nc = bass.Bass("TRN2", target_bir_lowering=False)
    NTOK = npt * 256
    NPRE = max(npre, 1) * 256

    def din(name, shape):
        return nc.dram_tensor(name, list(shape), F32, kind="ExternalInput").ap()

    def dout(name, shape):
        return nc.dram_tensor(name, list(shape), F32, kind="ExternalOutput").ap()

    xp = din("xp", [NTOK, D])
    xq = din("xq", [NPRE, D])
    flg = din("flg", [128, 1])
    xs = din("xs", [64, D])
    par = din("par", [128, NPAR])
    sst = din("sst", [2, 128, D])
    scv = din("scv", [128, NCC * 2 * 3])
    wd = {
        "w1g": din("w1g", [D, FF]), "w1u": din("w1u", [D, FF]), "w1d": din("w1d", [FF, D]),
        "win": din("win", [D, DIN]), "wout": din("wout", [2 * D, D]),
        "w2g": din("w2g", [D, FF]), "w2u": din("w2u", [D, FF]), "w2d": din("w2d", [FF, D]),
    }
    wv = {k: v.rearrange("(k p) n -> p k n", p=128) for k, v in wd.items()}
    yp = dout("yp", [NTOK, D])
    ys = dout("ys", [64, D])
    pcv = dout("pcv", [128, NCC * 3])
    pst = dout("pst", [128, D])
    sv = dout("sv", [64, D])
    scvo = dout("scvo", [128, NCC * 2 * 3])
    ssto = dout("ssto", [2, 128, D])
    NSLAB = 103
    wscr = nc.dram_tensor("wscr", [NSLAB, 128, 16 * SC], BF16, kind="Internal").ap()

    M = Mem(nc, 212480)
    PAR = M.alloc(NPAR, F32)
    X = M.alloc(2 * D, F32)
    XN = M.alloc(KC * TW, BF16)
    WS = [M.alloc(16 * SC, BF16) for _ in range(NSLOT)]
    S8a = M.alloc(D, BF16)
    S8b = M.alloc(D, F32)
    ST = M.alloc(D, F32)
    STB = M.alloc(D, BF16)
    WMT = M.alloc(1024, BF16)
    CV = M.alloc(NCC * 2 * 3, F32)
    SM = M.alloc(512, F32)
    FLG = M.alloc(8, F32)
    IDB = M.alloc(128, BF16)
    ov = M.top
    HT = M.alloc(FC * TW, BF16, at=ov)
    SG = [M.alloc(512, F32, at=ov + FC * TW * 2 + i * 2048) for i in range(2)]
    HK = [M.alloc(512, BF16, at=ov + FC * TW * 2 + 4096 + i * 1024) for i in range(4)]
    o2 = ov
    def oalloc(n, dt):
        nonlocal o2
        b = M.alloc(n, dt, at=o2)
        o2 += (n * (4 if dt == F32 else 2) + 31) // 32 * 32
        return b
    MIX = oalloc(32 * TW, BF16)
    YT = oalloc(2 * D, F32)
    o_ssd = o2
    XST = oalloc(2 * D, BF16)
    BFM = oalloc(4 * TW, BF16)
    CFM = oalloc(4 * TW, BF16)
    BTK = oalloc(2 * 512, BF16)
    XRAW = [oalloc(264, F32) for _ in range(2)]
    ACC = [oalloc(TW, F32) for _ in range(2)]
    XSF = [oalloc(TW, BF16) for _ in range(3)]
    DT4 = [oalloc(1024, F32) for _ in range(2)]
    GG = [oalloc(1024, BF16) for _ in range(2)]
    CBM = oalloc(512, F32)
    XDT = oalloc(D, BF16)
    XDW = oalloc(D, BF16)
    T1 = oalloc(512, F32)
    T2 = oalloc(512, F32)
    assert o2 <= M.nbytes, o2
    VNF = [M.alloc(512, F32, at=o_ssd + i * 2048) for i in range(2)]
    USB = [M.alloc(512, F32, at=o_ssd + 4096 + i * 2048) for i in range(2)]
    VNB = [M.alloc(512, BF16, at=o_ssd + 8192 + i * 1024) for i in range(2)]
    SZ = [M.alloc(512, F32, at=o_ssd + i * 2048) for i in range(2)]

    PSH = nc.alloc_psum_tensor("ps", [128, 4096], F32)
    PS = Buf(PSH, 4096, 0)
    PSB = Buf(PSH.bitcast(BF16), 8192, 0)

    c_ssq, c_std, c_rstd, c_ssq4 = 0, 8, 16, 24
    c_dt, c_dA, c_acs, c_nacs, c_eacs = 32, 96, 160, 192, 224
    c_wpre, c_w, c_dtw, c_edec = 256, 288, 320, 352
    c_sp1, c_sp2 = 384, 416
    c_A, c_eps = 448, 480

    P = Prog()
    dbg_seen = set()

    def dbgdump(name, apfn, shape, r, dt=F32):
        if name not in DBG or name in dbg_seen:
            return
        dbg_seen.add(name)
        dten = nc.dram_tensor("dbg_" + name, list(shape), dt, kind="ExternalOutput").ap()
        P.op("sp", lambda en: [en.dma_start(out=dten, in_=apfn())], r=r, w=[], group="dbg_" + name)

    def dbgdump_bf(name, apfn, n, r):
        if name not in DBG or name in dbg_seen:
            return
        P.op("dve", lambda en: en.tensor_copy(out=YT.ap(0, [(1, n)]), in_=apfn()), r=r, w=["dbgstage", ("yt", 0), ("yt", 1)])
        dbgdump(name, lambda: YT.ap(0, [(1, n)]), [128, n], ["dbgstage", ("yt", 0), ("yt", 1)])

    class BankAlloc:
        def __init__(self):
            self.i = 0

        def one(self):
            b = self.i
            self.i = (self.i + 1) % 8
            return b

        def two(self):
            if self.i % 2:
                self.i = (self.i + 1) % 8
            b = self.i
            self.i = (self.i + 2) % 8
            return b

    BK = BankAlloc()

    def pcol(c, n=1, np_=128):
        return PAR.ap(c, [(1, n)], np_=np_)

    def smc(c, n=1, np_=128):
        return SM.ap(c, [(1, n)], np_=np_)

    slab_specs = []
    slab_issued = [0]
    slab_next = [0]
    slab_stored = set()

    def ffn_specs(pre, base):
        s = []
        sid = base
        for j in range(FC // 4):
            s.append((sid, pre + "g", 0, 16, j * SC, SC)); sid += 1
            s.append((sid, pre + "u", 0, 16, j * SC, SC)); sid += 1
        for fb in range(4):
            for kq in range(4):
                s.append((sid, pre + "d", kq * 11, 11, fb * SC, SC)); sid += 1
        return s

    def mixer_specs(mode, with_c=False):
        s = []
        sid = 38
        for h2 in range(4):
            if mode == "full":
                s.append((sid, "win", 0, 16, 2048 + h2 * SC, SC))
                s.append((sid + 1, "win", 0, 16, h2 * SC, SC))
            sid += 2
        for i in range(6):
            if mode == "full" or i < 5 or with_c:
                s.append((sid, "win", 0, 16, 6144 + i * SC, SC))
            sid += 1
        s.append((sid, "win", 0, 16, 9216, 32)); sid += 1
        if mode == "full":
            for i in range(4):
                s.append((sid + i, "win", 0, 16, 4096 + i * SC, SC))
        sid += 4
        if mode == "full":
            for fb in range(4):
                for kq in range(2):
                    s.append((sid, "wout", kq * 16, 16, fb * SC, SC)); sid += 1
        return s

    def tile_specs(mode, with_c=False):
        if mode == "pre":
            return ffn_specs("w1", 0) + mixer_specs("pre", with_c)
        return ffn_specs("w1", 0) + mixer_specs("full") + ffn_specs("w2", 65)

    def issue_slab(i):
        sid, wname, k0, nk, c0, ncols = slab_specs[i]
        slot = i % NSLOT
        dst = WS[slot].ap(0, [(SC, nk), (1, ncols)])
        scr = wscr[sid].rearrange("p (k n) -> p k n", n=SC)[:, 0:nk, 0:ncols]
        if sid not in slab_stored:
            slab_stored.add(sid)
            src = wv[wname][:, k0:k0 + nk, c0:c0 + ncols]
            P.op("pool", lambda en, src=src, dst=dst: [en.dma_start(out=dst, in_=src)],
                 w=[("ws", slot)], group="ws%d" % slot)
            P.op("sp", lambda en, scr=scr, dst=dst: [en.dma_start(out=scr, in_=dst)],
                 r=[("ws", slot)], w=[("scr", sid)], group="scr%d" % slot)
        else:
            P.op("sp", lambda en, scr=scr, dst=dst: [en.dma_start(out=dst, in_=scr)],
                 r=[("scr", sid)], w=[("ws", slot)], group="ws%d" % slot)

    def next_slab(expect):
        i = slab_next[0]
        assert slab_specs[i][1] == expect[0] and slab_specs[i][4] == expect[1], (slab_specs[i], expect)
        while slab_issued[0] < min(len(slab_specs), i + NSLOT):
            issue_slab(slab_issued[0])
            slab_issued[0] += 1
        slab_next[0] += 1
        slot = i % NSLOT
        return WS[slot], ("ws", slot)

    def mm(out, lhsT, rhs, start, stop, r, w):
        P.op("pe", lambda en: en.matmul(out, lhsT=lhsT, rhs=rhs, start=start, stop=stop), r=r, w=w)

    def tp(out, in_, ident, r, w):
        P.op("pe", lambda en: en.transpose(out, in_, ident), r=r, w=w)

    def act(out, in_, func, r, w, bias=None, scale=None, accum=None):
        kw = {}
        if bias is not None:
            kw["bias"] = bias
        if scale is not None:
            kw["scale"] = scale
        if accum is not None:
            kw["accum_out"] = accum
        P.op("act", lambda en: en.activation(out=out, in_=in_, func=func, **kw), r=r, w=w)

    def tt(out, in0, in1, op, r, w):
        P.op("dve", lambda en: en.tensor_tensor(out=out, in0=in0, in1=in1, op=op), r=r, w=w)

    def ptt(out, in0, in1, op, r, w):
        P.op("pool", lambda en: en.tensor_tensor(out=out, in0=in0, in1=in1, op=op), r=r, w=w)

    def ts(out, in0, s1, s2, op0, op1, r, w):
        if op1 is None:
            P.op("dve", lambda en: en.tensor_scalar(out=out, in0=in0, scalar1=s1, scalar2=None, op0=op0), r=r, w=w)
        else:
            P.op("dve", lambda en: en.tensor_scalar(out=out, in0=in0, scalar1=s1, scalar2=s2, op0=op0, op1=op1), r=r, w=w)

    def stt(out, in0, scalar, in1, op0, op1, r, w):
        P.op("dve", lambda en: en.scalar_tensor_tensor(out=out, in0=in0, scalar=scalar, in1=in1, op0=op0, op1=op1), r=r, w=w)

    def dcopy(out, in_, r, w):
        P.op("dve", lambda en: en.tensor_copy(out=out, in_=in_), r=r, w=w)

    def recip(out, in_, r, w):
        P.op("dve", lambda en: en.reciprocal(out=out, in_=in_), r=r, w=w)

    def dma(eng, out, in_, r, w, group):
        P.op(eng, lambda en: [en.dma_start(out=out, in_=in_)], r=r, w=w, group=group)

    def bank(b, off, dims, np_=128):
        return PS.ap(b * 512 + off, dims, np_=np_)

    dma("sp", PAR.ap(0, [(1, NPAR)]), par, r=[], w=["par"], group="par")
    dma("sp", FLG.ap(0, [(1, 1)]), flg, r=[], w=["flg"], group="par")
    P.op("dve", lambda en: en.memset(smc(c_eps), EPS), w=["sm"])
    act(smc(c_A, 32), pcol(P_ALOG, 32), AF.Exp, r=["par"], w=["smA"])
    ts(smc(c_A, 32), smc(c_A, 32), -1.0, None, ALU.mult, None, r=["smA"], w=["smA"])
    tt(WMT.ap(0, [(128, 8), (1, 128)]), PAR.ap(P_WST, [(128, 8), (1, 128)]),
       PAR.ap(P_TRIU, [(0, 8), (1, 128)]), ALU.mult, r=["par"], w=["wmt"])
    dcopy(IDB.ap(0, [(1, 128)]), PAR.ap(P_ID, [(1, 128)]), r=["par"], w=["idb"])
    P.barrier()

    ident = lambda q: PAR.ap(P_ID, [(1, q)], np_=q)
    identb = lambda q: IDB.ap(0, [(1, q)], np_=q)

    def xtoks(tc):
        return [("x", tc, fb) for fb in range(4)]

    def normT(src_ap, src_tok, Q, tc, ngroups, gidx, dst, dst_c0, dst_tok):
        gs = D // ngroups
        for blk in range(4):
            act(S8b.ap(blk * 512, [(1, 512)], np_=Q), src_ap(blk * 512, 512), AF.Square,
                r=src_tok(blk), w=[("s8b", blk), ("ssq4", blk)], accum=smc(c_ssq4 + blk, 1, Q))
        if ngroups == 1:
            P.op("dve", lambda en: en.tensor_reduce(out=smc(c_ssq, 1, Q), in_=smc(c_ssq4, 4, Q),
                                                    axis=mybir.AxisListType.X, op=ALU.add),
                 r=[("ssq4", blk) for blk in range(4)], w=["ssq"])
            ssrc = c_ssq
            sr = ["ssq"]
        else:
            ssrc = c_ssq4
            sr = [("ssq4", blk) for blk in range(4)]
        act(smc(c_std, ngroups, Q), smc(ssrc, ngroups, Q), AF.Sqrt, r=sr, w=["std"],
            bias=smc(c_eps, 1, Q), scale=1.0 / gs)
        recip(smc(c_rstd, ngroups, Q), smc(c_std, ngroups, Q), r=["std"], w=["rstd"])
        for b in range(4):
            g = b if ngroups == 4 else 0
            ts(S8a.ap(b * 512, [(1, 512)], np_=Q), src_ap(b * 512, 512), smc(c_rstd + g, 1, Q), None,
               ALU.mult, None, r=src_tok(b) + ["rstd"], w=[("s8a", b)])
            bk = BK.one()
            for i in range(4):
                k = 4 * b + i
                tp(PSB.ap(bk * 1024 + i * Q, [(1, Q)]), S8a.ap(k * 128, [(1, 128)], np_=Q), identb(Q),
                   r=[("s8a", b)], w=[("ps", bk)])
            tt(dst.ap((dst_c0 + 4 * b) * TW + tc * Q, [(TW, 4), (1, Q)]),
               PSB.ap(bk * 1024, [(Q, 4), (1, Q)]),
               PAR.ap(P_GCOL + gidx * 16 + 4 * b, [(1, 4), (0, Q)]), ALU.mult,
               r=[("ps", bk)], w=[dst_tok(dst_c0 // 4 + b, tc)])

    def norm_x(Q, TC, gidx):
        for tc in range(TC):
            normT(lambda c0, n, tc=tc: X.ap(tc * D + c0, [(1, n)], np_=Q), lambda blk, tc=tc: [("x", tc, blk)], Q, tc, 1, gidx,
                  XN, 0, lambda kb, tc_: ("xn", kb, tc_))

    def xn_r(k, TC):
        return [("xn", k // 4, tc) for tc in range(TC)]

    def ffn(pre, Q, TC):
        def post(j, hks):
            for tc in range(TC):
                bt = BK.one()
                for i in range(4):
                    tp(PSB.ap(bt * 1024 + i * Q, [(1, Q)]), hks[tc].ap(i * 128, [(1, 128)], np_=Q), identb(Q),
                       r=[("hk", tc, j % 2)], w=[("ps", bt)])
                act(HT.ap(4 * j * TW + tc * Q, [(TW, 4), (1, Q)]), PSB.ap(bt * 1024, [(Q, 4), (1, Q)]), AF.Copy,
                    r=[("ps", bt)], w=[("h", j, tc)])

        pending = None
        for j in range(FC // 4):
            bg = [BK.one() for _ in range(TC)]
            bu = [BK.one() for _ in range(TC)]
            for (nm, bks) in (("g", bg), ("u", bu)):
                sl, tk = next_slab((pre + nm, j * SC))
                for tc in range(TC):
                    proj_tok(sl, tk, Q, tc, bks[tc], 0, SC)
            hks = [HK[2 * (j % 2) + tc] for tc in range(TC)]
            for tc in range(TC):
                sg = SG[tc]
                act(sg.ap(0, [(1, SC)], np_=Q), bank(bg[tc], 0, [(1, SC)], np_=Q), AF.Silu,
                    r=[("ps", bg[tc])], w=[("sg", tc)])
                tt(hks[tc].ap(0, [(1, SC)], np_=Q), sg.ap(0, [(1, SC)], np_=Q), bank(bu[tc], 0, [(1, SC)], np_=Q), ALU.mult,
                   r=[("sg", tc), ("ps", bu[tc])], w=[("hk", tc, j % 2)])
            if pending is not None:
                post(*pending)
            pending = (j, hks)
        post(*pending)
        for fb in range(4):
            bks = [BK.one() for _ in range(TC)]
            for kq in range(4):
                sl, tk = next_slab((pre + "d", fb * SC))
                for tc in range(TC):
                    for i in range(11):
                        jj = kq * 11 + i
                        mm(bank(bks[tc], 0, [(1, SC)], np_=Q), HT.ap(jj * TW + tc * Q, [(1, Q)]),
                           sl.ap(i * SC, [(1, SC)]), kq == 0 and i == 0, kq == 3 and i == 10,
                           r=[tk, ("h", jj // 4, tc)], w=[("ps", bks[tc])])
            for tc in range(TC):
                xa = X.ap(tc * D + fb * SC, [(1, SC)], np_=Q)
                stt(xa, bank(bks[tc], 0, [(1, SC)], np_=Q), 0.5, xa, ALU.mult, ALU.add,
                    r=[("ps", bks[tc]), ("x", tc, fb)], w=[("x", tc, fb)])

    def proj_tok(sl, tk, Q, tc, bk, c0, ncols):
        for k in range(KC):
            mm(bank(bk, c0, [(1, ncols)], np_=Q), XN.ap(k * TW + tc * Q, [(1, Q)]),
               sl.ap(k * SC, [(1, ncols)]), k == 0, k == KC - 1,
               r=[tk, ("xn", k // 4, tc)], w=[("ps", bk)])

    def mixer(job, Q, TC):
        T = Q * TC
        samp = job["kind"] == "sample"
        nseg = TC if samp else 1
        L = Q if samp else T
        triu = lambda n: PAR.ap(P_TRIU, [(1, n)], np_=n)
        pre = job["kind"] == "pre"
        if not pre:
            phase_barrier()
            for h2 in range(4):
                sl, tk = next_slab(("win", 2048 + h2 * SC))
                bkg = []
                for tc in range(TC):
                    bk = BK.one()
                    proj_tok(sl, tk, Q, tc, bk, 0, SC)
                    for hh in range(2):
                        act(S8b.ap(0, [(1, 256)], np_=Q), bank(bk, hh * 256, [(1, 256)], np_=Q), AF.Square,
                            r=[("ps", bk)], w=[("s8b", 0), "ssq"], accum=smc(c_ssq + hh, 1, Q))
                    act(smc(c_std, 2, Q), smc(c_ssq, 2, Q), AF.Sqrt, r=["ssq"], w=["std"],
                        bias=smc(c_eps, 1, Q), scale=1.0 / 256)
                    recip(smc(c_rstd, 2, Q), smc(c_std, 2, Q), r=["std"], w=["rstd"])
                    for hh in range(2):
                        stt(VNF[tc].ap(hh * 256, [(1, 256)], np_=Q), bank(bk, hh * 256, [(1, 256)], np_=Q),
                            smc(c_rstd + hh, 1, Q), PAR.ap(P_GV + h2 * SC + hh * 256, [(1, 256)], np_=Q),
                            ALU.mult, ALU.mult, r=[("ps", bk), "rstd"], w=[("vnf", tc)])
                    act(VNB[tc].ap(0, [(1, SC)], np_=Q), VNF[tc].ap(0, [(1, SC)], np_=Q), AF.Copy,
                        r=[("vnf", tc)], w=[("vnb", tc)])
                    if samp:
                        dma("sp", sv[tc * Q:(tc + 1) * Q, h2 * SC:(h2 + 1) * SC], VNF[tc].ap(0, [(1, SC)], np_=Q),
                            r=[("vnf", tc)], w=[], group="sv")
                sl, tk = next_slab(("win", h2 * SC))
                for tc in range(TC):
                    bk = BK.one()
                    proj_tok(sl, tk, Q, tc, bk, 0, SC)
                    act(USB[tc].ap(0, [(1, SC)], np_=Q), bank(bk, 0, [(1, SC)], np_=Q), AF.Copy,
                        r=[("ps", bk)], w=[("usb", tc)])
                for tc in range(TC):
                    bg = BK.one()
                    bkg.append(bg)
                    for hh in range(2):
                        h = 2 * h2 + hh
                        mm(bank(bg, hh * 256, [(1, 256)], np_=Q), WMT.ap(h * 128, [(1, Q)], np_=Q),
                           VNB[tc].ap(hh * 256, [(1, 256)], np_=Q), True, True, r=[("vnb", tc)], w=[("ps", bg)])
                for tc in range(TC):
                    for hh in range(2):
                        h = 2 * h2 + hh
                        stt(YT.ap(tc * D + h * 256, [(1, 256)], np_=Q), bank(bkg[tc], hh * 256, [(1, 256)], np_=Q),
                            pcol(P_BCOL + h, 1, Q), USB[tc].ap(hh * 256, [(1, 256)], np_=Q), ALU.add, ALU.mult,
                            r=[("ps", bkg[tc]), ("usb", tc)], w=[("yt", tc)])
            for tc in range(TC):
                normT(lambda c0, n, tc=tc: YT.ap(tc * D + c0, [(1, n)], np_=Q), lambda blk, tc=tc: [("yt", tc)], Q, tc, 1, G_GOUT,
                      MIX, 0, lambda kb, tc_: ("mx", kb, tc_))
            phase_barrier()

        tstate = {"tbk": None}

        def conv_stage2(c):
            xf = XSF[c % 3]
            xft = ("xsf", c % 3)
            if c % 4 == 0:
                tstate["tbk"] = [BK.one() for _ in range(TC)]
            tbk = tstate["tbk"]
            for tc in range(TC):
                tp(PSB.ap(tbk[tc] * 1024 + (c % 4) * 128, [(1, 128)], np_=Q), xf.ap(tc * Q, [(1, Q)]), identb(128),
                   r=[xft], w=[("ps", tbk[tc])])
            if c % 4 == 3:
                for tc in range(TC):
                    if c < 16:
                        dcopy(XST.ap(tc * D + (c - 3) * 128, [(1, 512)], np_=Q),
                              PSB.ap(tbk[tc] * 1024, [(1, 512)], np_=Q), r=[("ps", tbk[tc])], w=[("xst", tc)])
                    else:
                        dcopy(BTK.ap(tc * 512, [(1, 512)], np_=Q),
                              PSB.ap(tbk[tc] * 1024, [(1, 512)], np_=Q), r=[("ps", tbk[tc])], w=[("btk", tc)])

        pend = []
        for i in range(5 if (pre and not job.get("with_c")) else 6):
            sl, tk = next_slab(("win", 6144 + i * SC))
            for jj in range(4):
                c = 4 * i + jj
                bk = BK.one()
                for k in range(KC):
                    mm(bank(bk, 0, [(1, T)]), sl.ap(k * SC + jj * 128, [(1, 128)]), XN.ap(k * TW, [(1, T)]),
                       k == 0, k == KC - 1, r=[tk] + xn_r(k, TC), w=[("ps", bk)])
                xr = XRAW[c % 2]
                xrt = ("xraw", c % 2)
                W3 = L + 3
                act(xr.ap(0, [(W3, nseg), (1, 3)]), CV.ap(c * 6, [(3, nseg), (1, 3)]), AF.Copy,
                    r=[("cv", c)], w=[xrt])
                act(xr.ap(3, [(W3, nseg), (1, L)]), bank(bk, 0, [(L, nseg), (1, L)]), AF.Copy,
                    r=[("ps", bk)], w=[xrt])
                act(CV.ap(c * 6, [(3, nseg), (1, 3)]), xr.ap(L, [(W3, nseg), (1, 3)]), AF.Copy,
                    r=[xrt], w=[("cv", c)])
                ac = ACC[c % 2]
                act_ = ("acc", c % 2)
                aap = ac.ap(0, [(L, nseg), (1, L)])
                ts(aap, xr.ap(3, [(W3, nseg), (1, L)]), pcol(P_CONV + c * 5 + 3), pcol(P_CONV + c * 5 + 4),
                   ALU.mult, ALU.add, r=[xrt], w=[act_])
                for j in (2, 1, 0):
                    stt(aap, xr.ap(j, [(W3, nseg), (1, L)]), pcol(P_CONV + c * 5 + j), aap, ALU.mult, ALU.add,
                        r=[xrt, act_], w=[act_])
                if c < 20:
                    xf = XSF[c % 3]
                    xft = ("xsf", c % 3)
                    act(xf.ap(0, [(1, T)]), ac.ap(0, [(1, T)]), AF.Silu, r=[act_], w=[xft])
                    if c >= 16:
                        act(BFM.ap((c - 16) * TW, [(1, T)]), xf.ap(0, [(1, T)]), AF.Copy, r=[xft], w=["bfm"])
                    pend.append(c)
                else:
                    act(CFM.ap((c - 20) * TW, [(1, T)]), ac.ap(0, [(1, T)]), AF.Silu, r=[act_], w=["cfm"])
                while len(pend) > 2:
                    conv_stage2(pend.pop(0))
        while pend:
            conv_stage2(pend.pop(0))
        sl, tk = next_slab(("win", 9216))
        for tc in range(TC):
            bk = BK.one()
            proj_tok(sl, tk, Q, tc, bk, 0, 32)
            xa = smc(c_sp1, 32, Q)
            tt(xa, bank(bk, 0, [(1, 32)], np_=Q), pcol(P_DTB, 32, Q), ALU.add, r=[("ps", bk)], w=["sp1"])
            act(smc(c_sp2, 32, Q), xa, AF.Abs, r=["sp1"], w=["sp2"])
            act(smc(c_sp2, 32, Q), smc(c_sp2, 32, Q), AF.Exp, r=["sp2"], w=["sp2"], scale=-1.0)
            act(smc(c_sp2, 32, Q), smc(c_sp2, 32, Q), AF.Ln, r=["sp2"], w=["sp2"], bias=1.0)
            stt(smc(c_dt + tc * 32, 32, Q), xa, 0.0, smc(c_sp2, 32, Q), ALU.max, ALU.add,
                r=["sp1", "sp2"], w=[("dt", tc)])
            tt(smc(c_dA + tc * 32, 32, Q), smc(c_dt + tc * 32, 32, Q), smc(c_A, 32, Q), ALU.mult,
               r=[("dt", tc)], w=[("dA", tc)])
        HB = min(8, 512 // Q)
        for tc in range(TC):
            if samp:
                dma("sp", ST.ap(0, [(1, D)]), sst[tc], r=[], w=[("st", g_) for g_ in range(NG)], group="stin")
                act(STB.ap(0, [(1, D)]), ST.ap(0, [(1, D)]), AF.Copy, r=[("st", g_) for g_ in range(NG)], w=["stb"])
            elif job["first"] and tc == 0:
                P.op("dve", lambda en: en.memset(ST.ap(0, [(1, D)]), 0.0), w=[("st", g_) for g_ in range(NG)])
                P.op("dve", lambda en: en.memset(STB.ap(0, [(1, D)]), 0.0), w=["stb"])
            dAc = c_dA + tc * 32
            bk = BK.one()
            mm(bank(bk, 0, [(1, 32)], np_=Q), triu(Q), smc(dAc, 32, Q), True, True, r=[("dA", tc)], w=[("ps", bk)])
            dcopy(smc(c_acs, 32, Q), bank(bk, 0, [(1, 32)], np_=Q), r=[("ps", bk)], w=["acs"])
            ts(smc(c_nacs, 32, Q), bank(bk, 0, [(1, 32)], np_=Q), -1.0, None, ALU.mult, None, r=[("ps", bk)], w=["nacs"])
            act(smc(c_eacs, 32, Q), bank(bk, 0, [(1, 32)], np_=Q), AF.Exp, r=[("ps", bk)], w=["eacs"])
            if not pre:
                bk = BK.one()
                for g in range(NG):
                    mm(bank(bk, g * Q, [(1, Q)], np_=Q), BFM.ap(g * TW + tc * Q, [(1, Q)]),
                       CFM.ap(g * TW + tc * Q, [(1, Q)]), True, True, r=["bfm", "cfm"], w=[("ps", bk)])
                tt(CBM.ap(0, [(Q, 4), (1, Q)], np_=Q), bank(bk, 0, [(Q, 4), (1, Q)], np_=Q),
                   PAR.ap(P_TRIU, [(0, 4), (1, Q)], np_=Q), ALU.mult, r=[("ps", bk)], w=["cbm"])
                tt(XDT.ap(0, [(64, 32), (1, 64)], np_=Q), XST.ap(tc * D, [(64, 32), (1, 64)], np_=Q),
                   SM.ap(c_dt + tc * 32, [(1, 32), (0, 64)], np_=Q), ALU.mult, r=[("xst", tc), ("dt", tc)], w=["xdt"])
            def stage_a(g):
                d4 = DT4[g % 2]
                d4t = ("dt4", g % 2)
                gg = GG[g % 2]
                ggt = ("gg", g % 2)
                ptt(d4.ap(0, [(Q, 8), (1, Q)], np_=Q), PAR.ap(P_TRIU, [(0, 8), (1, Q)], np_=Q),
                    SM.ap(dAc + 8 * g, [(1, 8), (0, Q)], np_=Q), ALU.mult, r=[("dA", tc)], w=[d4t])
                rb = BK.two()
                nmm = 8 // HB
                for i in range(nmm):
                    mm(bank(rb + i, 0, [(1, HB * Q)]), PAR.ap(P_ONES, [(1, 128)], np_=Q),
                       d4.ap(i * HB * Q, [(1, HB * Q)], np_=Q), True, True, r=[d4t], w=[("ps", rb + i)])
                rtok = [("ps", rb + i) for i in range(nmm)]
                Rv = PS.ap(rb * 512, [(Q, 8), (1, Q)], np_=Q)
                if not pre:
                    tt(d4.ap(0, [(Q, 8), (1, Q)], np_=Q), Rv, SM.ap(c_acs + 8 * g, [(1, 8), (0, Q)], np_=Q),
                       ALU.min, r=rtok + ["acs"], w=[d4t])
                    for hh in range(8):
                        act(d4.ap(hh * Q, [(1, Q)], np_=Q), d4.ap(hh * Q, [(1, Q)], np_=Q), AF.Exp,
                            r=[d4t, "nacs"], w=[d4t], bias=smc(c_nacs + 8 * g + hh, 1, Q))
                    tt(gg.ap(0, [(Q, 8), (1, Q)], np_=Q), d4.ap(0, [(Q, 8), (1, Q)], np_=Q),
                       CBM.ap(g * Q, [(0, 8), (1, Q)], np_=Q), ALU.mult, r=[d4t, "cbm"], w=[ggt])
                tt(smc(c_wpre + 8 * g, 8, Q), PS.ap(rb * 512 + Q - 1, [(Q, 8)], np_=Q), smc(c_nacs + 8 * g, 8, Q), ALU.add,
                   r=rtok + ["nacs"], w=[("wpre", g)])
                act(smc(c_edec + 8 * g, 8), PS.ap(rb * 512 + Q - 1, [(Q, 8)], np_=128), AF.Exp, r=rtok, w=[("edec", g)])

            def stage_b(g):
                gg = GG[g % 2]
                ggt = ("gg", g % 2)
                bd = BK.one()
                for hh in range(8):
                    mm(bank(bd, hh * 64, [(1, 64)], np_=Q), gg.ap(hh * Q, [(1, Q)], np_=Q),
                       XDT.ap((8 * g + hh) * 64, [(1, 64)], np_=Q), True, True, r=[ggt, "xdt"], w=[("ps", bd)])
                bo = BK.one()
                mm(bank(bo, 0, [(1, 512)], np_=Q), CFM.ap(g * TW + tc * Q, [(1, Q)]), STB.ap(g * 512, [(1, 512)]),
                   True, True, r=["cfm", "stb"], w=[("ps", bo)])
                tt(T1.ap(0, [(64, 8), (1, 64)], np_=Q), bank(bo, 0, [(64, 8), (1, 64)], np_=Q),
                   SM.ap(c_eacs + 8 * g, [(1, 8), (0, 64)], np_=Q), ALU.mult, r=[("ps", bo), "eacs"], w=["t1"])
                tt(T1.ap(0, [(1, 512)], np_=Q), T1.ap(0, [(1, 512)], np_=Q), bank(bd, 0, [(1, 512)], np_=Q), ALU.add,
                   r=["t1", ("ps", bd)], w=["t1"])
                ptt(T2.ap(0, [(64, 8), (1, 64)], np_=Q), XST.ap(tc * D + g * 512, [(64, 8), (1, 64)], np_=Q),
                    PAR.ap(P_DSK + 8 * g, [(1, 8), (0, 64)], np_=Q), ALU.mult, r=[("xst", tc)], w=["t2"])
                tt(YT.ap(tc * D + g * 512, [(1, 512)], np_=Q), T1.ap(0, [(1, 512)], np_=Q), T2.ap(0, [(1, 512)], np_=Q),
                   ALU.add, r=["t1", "t2"], w=[("yt", tc)])

            stage_a(0)
            for g in range(NG):
                if g + 1 < NG:
                    stage_a(g + 1)
                if not pre:
                    stage_b(g)
            act(smc(c_w, 32, Q), smc(c_wpre, 32, Q), AF.Exp, r=[("wpre", g) for g in range(NG)], w=["w"])
            tt(smc(c_dtw, 32, Q), smc(c_w, 32, Q), smc(c_dt + tc * 32, 32, Q), ALU.mult, r=["w", ("dt", tc)], w=["dtw"])
            ptt(XDW.ap(0, [(64, 32), (1, 64)], np_=Q), XST.ap(tc * D, [(64, 32), (1, 64)], np_=Q),
                SM.ap(c_dtw, [(1, 32), (0, 64)], np_=Q), ALU.mult, r=[("xst", tc), "dtw"], w=["xdw"])
            for g in range(NG):
                bs = BK.one()
                mm(bank(bs, 0, [(1, 512)]), BTK.ap(tc * 512 + g * 128, [(1, 128)], np_=Q),
                   XDW.ap(g * 512, [(1, 512)], np_=Q), True, True, r=[("btk", tc), "xdw"], w=[("ps", bs)])
                sa = ST.ap(g * 512, [(64, 8), (1, 64)])
                ptt(sa, sa, SM.ap(c_edec + 8 * g, [(1, 8), (0, 64)]), ALU.mult, r=[("st", g), ("edec", g)], w=[("st", g)])
                tt(ST.ap(g * 512, [(1, 512)]), ST.ap(g * 512, [(1, 512)]), bank(bs, 0, [(1, 512)]), ALU.add,
                   r=[("st", g), ("ps", bs)], w=[("st", g)])
            act(STB.ap(0, [(1, D)]), ST.ap(0, [(1, D)]), AF.Copy, r=[("st", g_) for g_ in range(NG)], w=["stb"])
            if samp:
                dma("sp", ssto[tc], ST.ap(0, [(1, D)]), r=[("st", g_) for g_ in range(NG)], w=[], group="stout")
        if (not samp) and job["last"]:
            dma("sp", pst, ST.ap(0, [(1, D)]), r=[("st", g_) for g_ in range(NG)], w=[], group="stout")
            dma("sp", pcv.rearrange("p (c r) -> p c r", r=3), CV.ap(0, [(6, NCC), (1, 3)]), r=[("cv", c) for c in range(NCC)], w=[], group="cvout")
        if samp:
            dma("sp", scvo, CV.ap(0, [(1, NCC * 6)]), r=[("cv", c) for c in range(NCC)], w=[], group="cvout")
        if pre:
            return
        phase_barrier()
        for i in range(4):
            sl, tk = next_slab(("win", 4096 + i * SC))
            for tc in range(TC):
                bk = BK.one()
                proj_tok(sl, tk, Q, tc, bk, 0, SC)
                act(SZ[tc].ap(0, [(1, SC)], np_=Q), bank(bk, 0, [(1, SC)], np_=Q), AF.Silu, r=[("ps", bk)], w=[("sz", tc)])
                ya = YT.ap(tc * D + i * SC, [(1, SC)], np_=Q)
                tt(ya, ya, SZ[tc].ap(0, [(1, SC)], np_=Q), ALU.mult, r=[("yt", tc), ("sz", tc)], w=[("yt", tc)])
        for tc in range(TC):
            normT(lambda c0, n, tc=tc: YT.ap(tc * D + c0, [(1, n)], np_=Q), lambda blk, tc=tc: [("yt", tc)], Q, tc, 4, G_SSM,
                  MIX, 16, lambda kb, tc_: ("mx", kb, tc_))
        for fb in range(4):
            bks = [BK.one() for _ in range(TC)]
            for kq in range(2):
                sl, tk = next_slab(("wout", fb * SC))
                for tc in range(TC):
                    for i in range(16):
                        kk = kq * 16 + i
                        mm(bank(bks[tc], 0, [(1, SC)], np_=Q), MIX.ap(kk * TW + tc * Q, [(1, Q)]),
                           sl.ap(i * SC, [(1, SC)]), kq == 0 and i == 0, kq == 1 and i == 15,
                           r=[tk, ("mx", kk // 4, tc)], w=[("ps", bks[tc])])
            for tc in range(TC):
                xa = X.ap(tc * D + fb * SC, [(1, SC)], np_=Q)
                tt(xa, xa, bank(bks[tc], 0, [(1, SC)], np_=Q), ALU.add,
                   r=[("ps", bks[tc]), ("x", tc, fb)], w=[("x", tc, fb)])

    def phase_barrier():
        engs = ("pe", "act", "dve")
        lasts = [st[-1] for e, st in P.streams.items() if st and e in engs]
        lasts = [o for o in lasts if o.group is None]
        for o in reversed(P.streams["pool"]):
            if o.group is None:
                lasts.append(o)
                break
        gl = [o for g, o in P.glast.items() if not (str(g).startswith("ws") or str(g).startswith("scr"))]
        for e in engs:
            P.op(e, lambda en: en.nop(), extra=[o for o in lasts if o.eng != e] + gl)

    def tile(job):
        samp = job["kind"] == "sample"
        pre = job["kind"] == "pre"
        Q = 32 if samp else 128
        TC = 2
        allx = [t for tc in range(TC) for t in xtoks(tc)]
        if samp:
            dma("sp", X.ap(0, [(D, TC), (1, D)], np_=Q), xs.rearrange("(c p) d -> p c d", p=Q), r=[], w=allx, group="xin")
            dma("sp", CV.ap(0, [(1, NCC * 6)]), scv, r=[], w=[("cv", c) for c in range(NCC)], group="cvin")
        else:
            r0 = job["row0"]
            src = (xq if pre else xp)[r0:r0 + 256, :].rearrange("(c p) d -> p c d", p=Q)
            dma("sp", X.ap(0, [(D, TC), (1, D)], np_=Q), src, r=[], w=allx, group="xin")
            if job["first"]:
                P.op("dve", lambda en: en.memset(CV.ap(0, [(1, NCC * 6)]), 0.0), w=[("cv", c) for c in range(NCC)])
            if job.get("apply_flag"):
                ts(CV.ap(0, [(1, NCC * 6)]), CV.ap(0, [(1, NCC * 6)]), FLG.ap(0, [(1, 1)]), None, ALU.mult, None,
                   r=[("cv", c) for c in range(NCC)] + ["flg"], w=[("cv", c) for c in range(NCC)])
                ts(ST.ap(0, [(1, D)]), ST.ap(0, [(1, D)]), FLG.ap(0, [(1, 1)]), None, ALU.mult, None,
                   r=[("st", g_) for g_ in range(NG)] + ["flg"], w=[("st", g_) for g_ in range(NG)])
                act(STB.ap(0, [(1, D)]), ST.ap(0, [(1, D)]), AF.Copy, r=[("st", g_) for g_ in range(NG)], w=["stb"])
        allxt = allx
        dbgdump("x0", lambda: X.ap(0, [(1, 2 * D)]), [128, 2 * D], allxt)
        norm_x(Q, TC, G_FFN1)
        ffn("w1", Q, TC)
        dbgdump("x1", lambda: X.ap(0, [(1, 2 * D)]), [128, 2 * D], allxt)
        norm_x(Q, TC, G_MIX)
        mixer(job, Q, TC)
        if pre:
            return
        norm_x(Q, TC, G_FFN2)
        ffn("w2", Q, TC)
        for tc in range(TC):
            s8all = ["s8b"] + [("s8b", blk) for blk in range(4)]
            act(S8b.ap(0, [(1, D)], np_=Q), X.ap(tc * D, [(1, D)], np_=Q), AF.Square, r=xtoks(tc), w=s8all + ["ssq"],
                accum=smc(c_ssq, 1, Q))
            act(smc(c_std, 1, Q), smc(c_ssq, 1, Q), AF.Sqrt, r=["ssq"], w=["std"], bias=smc(c_eps, 1, Q), scale=1.0 / D)
            recip(smc(c_rstd, 1, Q), smc(c_std, 1, Q), r=["std"], w=["rstd"])
            stt(S8b.ap(0, [(1, D)], np_=Q), X.ap(tc * D, [(1, D)], np_=Q), smc(c_rstd, 1, Q),
                PAR.ap(P_GFIN, [(1, D)], np_=Q), ALU.mult, ALU.mult, r=xtoks(tc) + ["rstd"], w=s8all)
            if samp:
                dst = ys[tc * Q:(tc + 1) * Q, :]
            else:
                dst = yp[job["row0"] + tc * Q: job["row0"] + (tc + 1) * Q, :]
            dma("sp", dst, S8b.ap(0, [(1, D)], np_=Q), r=s8all, w=[], group="yout")

    jobs = [dict(kind="pre", row0=i * 256, first=(i == 0), last=False, with_c=(i == npre - 1)) for i in range(npre)]
    jobs += [dict(kind="prompt", row0=i * 256, first=(i == 0 and npre == 0), last=(i == npt - 1),
                  apply_flag=(i == 0 and npre > 0)) for i in range(npt)]
    if sample:
        jobs.append(dict(kind="sample"))
    for job in jobs:
        slab_specs.extend(tile_specs("pre" if job["kind"] == "pre" else "full", job.get("with_c", False)))
    for job in jobs:
        tile(job)
    assert slab_next[0] == len(slab_specs)
    P.finish()
    P.emit(nc)
    return nc


def _pack_params(inp):
    par = np.zeros((128, NPAR), np.float32)
    for i, nm in enumerate(["ffn1_norm", "mix_norm", "ffn2_norm", "gmlp_out_norm", "ssm_out_norm"]):
        par[:, P_GCOL + i * 16:P_GCOL + (i + 1) * 16] = inp[nm][0].reshape(16, 128).T
    cw = inp["conv_w"][0]
    cb = inp["conv_b"][0]
    cc = np.concatenate([cw, cb[None]], 0)
    par[:, P_CONV:P_CONV + 120] = cc.reshape(5, NCC, 128).transpose(2, 1, 0).reshape(128, 120)
    par[:, P_BCOL:P_BCOL + 8] = inp["gmlp_b_s"][0].T
    par[:, P_DTB:P_DTB + 32] = inp["dt_bias"][0][None]
    par[:, P_ALOG:P_ALOG + 32] = inp["a_log"][0][None]
    par[:, P_DSK:P_DSK + 32] = inp["d_skip"][0][None]
    par[:, P_ID:P_ID + 128] = np.eye(128, dtype=np.float32)
    par[:, P_TRIU:P_TRIU + 128] = np.triu(np.ones((128, 128), np.float32))
    par[:, P_ONES:P_ONES + 128] = 1.0
    par[:, P_WST:P_WST + 1024] = inp["gmlp_w_s"][0].transpose(2, 0, 1).reshape(128, 1024)
    par[:, P_GFIN:P_GFIN + D] = inp["final_norm"][None]
    par[:, P_GV:P_GV + D] = inp["gmlp_v_norm"][0][None]
    return par


_CACHE = {}


def _run(inp, npt, npre, ncores, trace=False):
    key = (npt, npre)
    if key not in _CACHE:
        _CACHE[key] = build(npt, npre)
    nc = _CACHE[key]
    par = _pack_params(inp)
    w = dict(w1g=inp["ffn1_w_gate"][0], w1u=inp["ffn1_w_up"][0], w1d=inp["ffn1_w_down"][0],
             win=inp["w_in"][0], wout=inp["w_out"][0],
             w2g=inp["ffn2_w_gate"][0], w2u=inp["ffn2_w_up"][0], w2d=inp["ffn2_w_down"][0])
    w = {k: np.ascontiguousarray(v) for k, v in w.items()}
    nb = inp["x_prompt"].shape[0]
    n = npt * 256
    in_maps = []
    for c in range(ncores):
        m = dict(w)
        if npre > 0:
            b, half = (c // 2) % nb, c % 2
            m["xp"] = np.ascontiguousarray(inp["x_prompt"][b, half * n:(half + 1) * n])
            m["xq"] = np.ascontiguousarray(inp["x_prompt"][b, 0:n])
            m["flg"] = np.full((128, 1), float(half), np.float32)
        else:
            b = c % nb
            m["xp"] = np.ascontiguousarray(inp["x_prompt"][b, :n])
            m["xq"] = np.ascontiguousarray(inp["x_prompt"][b, :256])
            m["flg"] = np.zeros((128, 1), np.float32)
        m["xs"] = np.ascontiguousarray(inp["x_sample"][2 * c:2 * c + 2].reshape(64, D))
        m["par"] = par
        m["sst"] = np.ascontiguousarray(inp["state_ssm"][0, 2 * c:2 * c + 2].reshape(2, D, 128).transpose(0, 2, 1))
        m["scv"] = np.ascontiguousarray(
            inp["state_conv"][0, 2 * c:2 * c + 2].reshape(2, 3, NCC, 128).transpose(3, 2, 0, 1).reshape(128, NCC * 6))
        in_maps.append(m)
    res = run_bass_kernel_spmd(nc, in_maps, core_ids=list(range(ncores)), trace=trace)
    return res


def _assemble(res, npt, npre, ncores, nb):
    r = res.results
    if npre > 0:
        y_prompt = np.stack([np.concatenate([r[2 * b]["yp"], r[2 * b + 1]["yp"]]) for b in range(nb)])
        last = [2 * b + 1 for b in range(nb)]
    else:
        y_prompt = np.stack([r[b]["yp"] for b in range(nb)])
        last = list(range(nb))
    y_sample = np.concatenate([r[c]["ys"].reshape(2, 32, D) for c in range(ncores)])
    p_conv = np.stack([r[c]["pcv"].reshape(128, NCC, 3).transpose(2, 1, 0).reshape(3, NCC * 128) for c in last])[None]
    p_ssm = np.stack([r[c]["pst"].T.reshape(NH, HD, 128) for c in last])[None]
    s_v = np.concatenate([r[c]["sv"].reshape(2, 32, D) for c in range(ncores)])[None]
    s_conv = np.concatenate([r[c]["scvo"].reshape(128, NCC, 2, 3).transpose(2, 3, 1, 0).reshape(2, 3, NCC * 128)
                             for c in range(ncores)])[None]
    s_ssm = np.concatenate([r[c]["ssto"].transpose(0, 2, 1).reshape(2, NH, HD, 128) for c in range(ncores)])[None]
    return (y_prompt, y_sample, p_conv, p_ssm, s_v, s_conv, s_ssm)


def kernel(**inputs):
    inp = {k: np.asarray(v) for k, v in inputs.items()}
    nb = inp["x_prompt"].shape[0]
    npt = inp["x_prompt"].shape[1] // 512
    res = _run(inp, npt, npt, 8)
    outs = _assemble(res, npt, npt, 8, nb)
    return tuple(np.ascontiguousarray(o, dtype=np.float32) for o in outs)
```
